# Optimizing a Trainium2 kernel written in Bass

```python
import math
import jax, jax.numpy as jnp
from jax import lax
import numpy as np

D_MODEL = 2048
BATCH = 1
SEQ = 8192
DEPTH = 4

HEAD_DIM = 128
N_HEADS_TOTAL = D_MODEL // HEAD_DIM
N_HEADS_C = N_HEADS_TOTAL // 4
N_HEADS_A = (N_HEADS_TOTAL - N_HEADS_C) // 2
N_HEADS_B = N_HEADS_TOTAL - N_HEADS_C - N_HEADS_A
DIFF_QK_DIM = HEAD_DIM // 2
WIDTH_A = N_HEADS_A * HEAD_DIM
WIDTH_B = N_HEADS_B * HEAD_DIM
WIDTH_C = N_HEADS_C * HEAD_DIM
MIX_WIDTH = WIDTH_A + WIDTH_B + WIDTH_C
IN_COLS = 3 * WIDTH_A + 3 * WIDTH_B + 2 * WIDTH_C
DILATION_PATTERNS = ((128, 1), (512, 4), (2048, 16))
BLK = 128
CHUNK = 128
ROPE_THETA = 500000.0
ROPE_FRACTION = 4
FFN_DIM = 5632
CONV_WIDTH = 3
EPS = 1e-6

kernel_name = "hybrid_dilated_diff_sgu_convffn"


def rmsnorm(x, g):
    xf = x.astype(jnp.float32)
    y = xf * lax.rsqrt(jnp.mean(xf * xf, axis=-1, keepdims=True) + EPS)
    return (y * g.astype(jnp.float32)).astype(x.dtype)


def layernorm(x, g, b):
    xf = x.astype(jnp.float32)
    mu = jnp.mean(xf, axis=-1, keepdims=True)
    var = jnp.mean(jnp.square(xf - mu), axis=-1, keepdims=True)
    y = (xf - mu) * lax.rsqrt(var + EPS)
    return (y * g.astype(jnp.float32) + b.astype(jnp.float32)).astype(x.dtype)


def rope_tables(seq, head_dim):
    rot_dim = head_dim // ROPE_FRACTION
    inv = 1.0 / (ROPE_THETA ** (jnp.arange(0, rot_dim, 2, dtype=jnp.float32) / rot_dim))
    ang = jnp.arange(seq, dtype=jnp.float32)[:, None] * inv[None, :]
    return jnp.cos(ang), jnp.sin(ang)


def rope_partial(x, cos, sin):
    rd = cos.shape[-1] * 2
    xr = x[..., :rd].astype(jnp.float32)
    x1, x2 = xr[..., : rd // 2], xr[..., rd // 2:]
    c = cos[None, :, None, :]
    s = sin[None, :, None, :]
    rot = jnp.concatenate([x1 * c - x2 * s, x2 * c + x1 * s], axis=-1)
    return jnp.concatenate([rot.astype(x.dtype), x[..., rd:]], axis=-1)


def banded_window_attn(q, k, v, window):
    N, L, H, dh = q.shape
    nb = L // BLK
    scale = dh ** -0.5
    qb = q.reshape(N, nb, BLK, H, dh)
    kb = k.reshape(N, nb, BLK, H, dh)
    vb = v.reshape(N, nb, BLK, H, dh)
    pad = ((0, 0), (1, 0), (0, 0), (0, 0), (0, 0))
    kcat = jnp.concatenate([jnp.pad(kb, pad)[:, :-1], kb], axis=2)
    vcat = jnp.concatenate([jnp.pad(vb, pad)[:, :-1], vb], axis=2)
    s = jnp.einsum('nbqhd,nbkhd->nbhqk', qb, kcat, preferred_element_type=jnp.float32) * scale
    qi = jnp.arange(BLK)[:, None] + BLK
    kj = jnp.arange(2 * BLK)[None, :]
    dist = qi - kj
    band = (dist >= 0) & (dist <= window)
    has_prev = (jnp.arange(nb) > 0)[:, None, None] | (kj >= BLK)[None]
    mask = band[None] & has_prev
    s = jnp.where(mask[None, :, None], s, -jnp.inf)
    m = jnp.max(s, axis=-1, keepdims=True)
    p = jnp.exp(s - m)
    l = jnp.sum(p, axis=-1)
    o = jnp.einsum('nbhqk,nbkhd->nbqhd', p.astype(v.dtype), vcat, preferred_element_type=jnp.float32)
    o = o / jnp.transpose(l, (0, 1, 3, 2))[..., None]
    lse = jnp.transpose(m[..., 0] + jnp.log(l), (0, 1, 3, 2))
    return o.reshape(N, L, H, dh), lse.reshape(N, L, H)


def dilated_window_attn(q, k, v, window, dilation):
    B, S, H, dh = q.shape
    span = dilation * BLK
    Sp = -(-S // span) * span
    Ls = Sp // dilation

    def to_sub(t):
        t = jnp.pad(t, ((0, 0), (0, Sp - S), (0, 0), (0, 0)))
        return t.reshape(B, Ls, dilation, H, dh).transpose(0, 2, 1, 3, 4).reshape(B * dilation, Ls, H, dh)

    o, lse = banded_window_attn(to_sub(q), to_sub(k), to_sub(v), window // dilation)
    o = o.reshape(B, dilation, Ls, H, dh).transpose(0, 2, 1, 3, 4).reshape(B, Sp, H, dh)[:, :S]
    lse = lse.reshape(B, dilation, Ls, H).transpose(0, 2, 1, 3).reshape(B, Sp, H)[:, :S]
    return o, lse


def dilated_mixture(q, k, v):
    outs, lses = [], []
    for window, dilation in DILATION_PATTERNS:
        o, lse = dilated_window_attn(q, k, v, window, dilation)
        outs.append(o)
        lses.append(lse)
    w = jax.nn.softmax(jnp.stack(lses, axis=0), axis=0)
    o = jnp.sum(w[..., None] * jnp.stack(outs, axis=0), axis=0)
    return o.astype(q.dtype)


def diff_attention(q1, q2, k1, k2, v, lam):
    B, S, H, dq = q1.shape
    dv = v.shape[-1]
    nq = S // BLK
    scale = dq ** -0.5
    kpos = jnp.arange(S)

    def split(t):
        return t.reshape(B, nq, BLK, H, t.shape[-1]).transpose(1, 0, 2, 3, 4)

    def step(args):
        qa, qb, bi = args
        qpos = bi * BLK + jnp.arange(BLK)
        mask = qpos[:, None] >= kpos[None, :]

        def probs(qq, kk):
            s = jnp.einsum('bqhd,bkhd->bhqk', qq, kk, preferred_element_type=jnp.float32) * scale
            return jax.nn.softmax(jnp.where(mask, s, -jnp.inf), axis=-1)

        p = probs(qa, k1) - lam * probs(qb, k2)
        return jnp.einsum('bhqk,bkhd->bqhd', p.astype(v.dtype), v, preferred_element_type=jnp.float32).astype(v.dtype)

    out = lax.map(step, (split(q1), split(q2), jnp.arange(nq)))
    return out.transpose(1, 0, 2, 3, 4).reshape(B, S, H, dv)


def spatial_gating(u, v, ln_g, ln_b, w_s, b_s):
    B, S, _ = u.shape
    nc = S // CHUNK
    vn = layernorm(v, ln_g, ln_b).reshape(B, nc, CHUNK, N_HEADS_C, HEAD_DIM)
    tri = jnp.tril(jnp.ones((CHUNK, CHUNK), dtype=bool))
    wm = jnp.where(tri[None], w_s, jnp.zeros((), w_s.dtype))
    y = jnp.einsum('gij,bcjge->bcige', wm, vn) + jnp.transpose(b_s)[None, None, :, :, None]
    return u * y.reshape(B, S, WIDTH_C)


def conv_gated_mlp(h, w_up, conv_w, conv_b, w_down):
    up = h @ w_up
    hp = jnp.pad(up, ((0, 0), (CONV_WIDTH - 1, 0), (0, 0)))
    S = up.shape[1]
    c = conv_b + sum(conv_w[i] * hp[:, i:i + S] for i in range(CONV_WIDTH))
    gate, val = jnp.split(c, 2, axis=-1)
    return (jax.nn.silu(gate) * val) @ w_down


def setup_inputs(seed: int = 0) -> dict:
    key = jax.random.key(seed)
    ks = jax.random.split(key, 20)
    f = jnp.float32
    nrm = lambda k, shape, s: jax.random.normal(k, shape, f) * s
    return {
        "x": nrm(ks[0], (BATCH, SEQ, D_MODEL), 1.0),
        "norm_mix": 1.0 + nrm(ks[1], (DEPTH, D_MODEL), 0.02),
        "w_in": nrm(ks[2], (DEPTH, D_MODEL, IN_COLS), D_MODEL ** -0.5),
        "lambda_q1": nrm(ks[3], (DEPTH, DIFF_QK_DIM), 0.1),
        "lambda_k1": nrm(ks[4], (DEPTH, DIFF_QK_DIM), 0.1),
        "lambda_q2": nrm(ks[5], (DEPTH, DIFF_QK_DIM), 0.1),
        "lambda_k2": nrm(ks[6], (DEPTH, DIFF_QK_DIM), 0.1),
        "diff_subln": 1.0 + nrm(ks[7], (DEPTH, HEAD_DIM), 0.02),
        "sgu_ln_g": 1.0 + nrm(ks[8], (DEPTH, WIDTH_C), 0.02),
        "sgu_ln_b": nrm(ks[9], (DEPTH, WIDTH_C), 0.02),
        "sgu_w": nrm(ks[10], (DEPTH, N_HEADS_C, CHUNK, CHUNK), CHUNK ** -0.5),
        "sgu_b": 1.0 + nrm(ks[11], (DEPTH, N_HEADS_C, CHUNK), 0.02),
        "w_out": nrm(ks[12], (DEPTH, MIX_WIDTH, D_MODEL), MIX_WIDTH ** -0.5),
        "norm_ffn": 1.0 + nrm(ks[13], (DEPTH, D_MODEL), 0.02),
        "w_up": nrm(ks[14], (DEPTH, D_MODEL, 2 * FFN_DIM), D_MODEL ** -0.5),
        "conv_w": nrm(ks[15], (DEPTH, CONV_WIDTH, 2 * FFN_DIM), CONV_WIDTH ** -0.5),
        "conv_b": nrm(ks[16], (DEPTH, 2 * FFN_DIM), 0.01),
        "w_down": nrm(ks[17], (DEPTH, FFN_DIM, D_MODEL), FFN_DIM ** -0.5),
        "norm_final": 1.0 + nrm(ks[18], (D_MODEL,), 0.02),
    }


def reference(x, norm_mix, w_in, lambda_q1, lambda_k1, lambda_q2, lambda_k2, diff_subln,
              sgu_ln_g, sgu_ln_b, sgu_w, sgu_b, w_out, norm_ffn, w_up, conv_w, conv_b, w_down,
              norm_final):
    B, S, _ = x.shape
    cos_a, sin_a = rope_tables(S, HEAD_DIM)
    cos_b, sin_b = rope_tables(S, DIFF_QK_DIM)
    splits = np.cumsum([WIDTH_A, WIDTH_A, WIDTH_A, WIDTH_B, WIDTH_B, WIDTH_B, WIDTH_C])
    for l in range(DEPTH):
        h = rmsnorm(x, norm_mix[l])
        proj = h @ w_in[l]
        qa, ka, va, qb, kb, vb, u, v = jnp.split(proj, splits, axis=-1)
        qa = rope_partial(qa.reshape(B, S, N_HEADS_A, HEAD_DIM), cos_a, sin_a)
        ka = rope_partial(ka.reshape(B, S, N_HEADS_A, HEAD_DIM), cos_a, sin_a)
        va = va.reshape(B, S, N_HEADS_A, HEAD_DIM)
        out_a = dilated_mixture(qa, ka, va).reshape(B, S, WIDTH_A)
        qb = qb.reshape(B, S, N_HEADS_B, 2, DIFF_QK_DIM)
        kb = kb.reshape(B, S, N_HEADS_B, 2, DIFF_QK_DIM)
        q1 = rope_partial(qb[..., 0, :], cos_b, sin_b)
        q2 = rope_partial(qb[..., 1, :], cos_b, sin_b)
        k1 = rope_partial(kb[..., 0, :], cos_b, sin_b)
        k2 = rope_partial(kb[..., 1, :], cos_b, sin_b)
        lambda_init = 0.8 - 0.6 * math.exp(-0.3 * l)
        lam = (jnp.exp(jnp.sum(lambda_q1[l].astype(jnp.float32) * lambda_k1[l].astype(jnp.float32)))
               - jnp.exp(jnp.sum(lambda_q2[l].astype(jnp.float32) * lambda_k2[l].astype(jnp.float32)))
               + lambda_init)
        ob = diff_attention(q1, q2, k1, k2, vb.reshape(B, S, N_HEADS_B, HEAD_DIM), lam)
        out_b = (rmsnorm(ob, diff_subln[l]) * (1.0 - lambda_init)).reshape(B, S, WIDTH_B)
        out_c = spatial_gating(jax.nn.gelu(u, approximate=False), jax.nn.gelu(v, approximate=False),
                               sgu_ln_g[l], sgu_ln_b[l], sgu_w[l], sgu_b[l])
        mix = jnp.concatenate([out_a, out_b.astype(x.dtype), out_c], axis=-1)
        x = x + mix @ w_out[l]
        x = x + conv_gated_mlp(rmsnorm(x, norm_ffn[l]), w_up[l], conv_w[l], conv_b[l], w_down[l])
    return rmsnorm(x, norm_final)
```

```python
import math
from contextlib import ExitStack

import numpy as np
import ml_dtypes
import concourse.bass as bass
import concourse.mybir as mybir
from concourse.bass_utils import run_bass_kernel_spmd

F32 = mybir.dt.float32
BF16 = mybir.dt.bfloat16
AF = mybir.ActivationFunctionType
ALU = mybir.AluOpType

D = 2048
SEQ = 8192
NC = 8
NB = 8
DEPTH = 4
INC = 5632
FF = 5632
EPS = 1e-6
VW = 132
KVW = 1024 + NB * VW
NL = DEPTH
STOP = None


import types


def _freeze(fn):
    if fn.__closure__ is None:
        return fn
    cells = []
    for c in fn.__closure__:
        try:
            cells.append(types.CellType(c.cell_contents))
        except ValueError:
            cells.append(c)
    return types.FunctionType(fn.__code__, fn.__globals__, fn.__name__, fn.__defaults__, tuple(cells))


class Buf:
    __slots__ = ("name", "w", "r")

    def __init__(self, name):
        self.name = name
        self.w = None
        self.r = []


class Sched:
    ENGS = ("tensor", "vector", "scalar", "gpsimd", "sync")

    def __init__(self, nc, es):
        self.nc = nc
        self.es = es
        self.q = {e: [] for e in self.ENGS}
        self.sem = {e: es.enter_context(nc.semaphore("s_" + e)) for e in self.ENGS}
        self.cnt = {e: 0 for e in self.ENGS}
        self.waited = {e: {} for e in self.ENGS}
        self.dsem = {}
        self.dcnt = {}

    def _wait(self, eng, tok):
        if tok is None:
            return
        key, sem, val = tok
        if key == eng and eng == "tensor":
            return
        if key in self.cnt:
            assert val <= self.cnt[key], f"wait on future inc {key} {val} > {self.cnt[key]}"
        else:
            assert val <= self.dcnt[key]
        if self.waited[eng].get(key, 0) >= val:
            return
        self.waited[eng][key] = val
        self.q[eng].append(("wait", sem, val))

    def _deps(self, eng, reads, writes):
        for b in reads:
            self._wait(eng, b.w)
        for b in writes:
            self._wait(eng, b.w)
            for t in b.r:
                self._wait(eng, t)

    def _commit(self, tok, reads, writes):
        for b in writes:
            b.w = tok
            b.r = []
        for b in reads:
            b.r.append(tok)
            if len(b.r) > 16:
                best = {}
                for t in b.r:
                    if t[0] not in best or best[t[0]][2] < t[2]:
                        best[t[0]] = t
                b.r = list(best.values())

    def op(self, eng, fn, reads=(), writes=(), signal=True):
        fn = _freeze(fn)
        self._deps(eng, reads, writes)
        if signal:
            self.cnt[eng] += 1
            tok = (eng, self.sem[eng], self.cnt[eng])
            self.q[eng].append(("op", fn, self.sem[eng], 1))
        else:
            tok = (eng, self.sem[eng], self.cnt[eng] + 1)
            self.q[eng].append(("op", fn, None, 0))
        self._commit(tok, reads, writes)
        return tok

    def dma(self, qeng, slot, fn, reads=(), writes=(), inc=16):
        fn = _freeze(fn)
        if slot not in self.dsem:
            self.dsem[slot] = self.es.enter_context(self.nc.semaphore("d_" + slot))
            self.dcnt[slot] = 0
        self._deps(qeng, reads, writes)
        self.dcnt[slot] += inc
        tok = (slot, self.dsem[slot], self.dcnt[slot])
        self.q[qeng].append(("op", fn, self.dsem[slot], inc))
        self._commit(tok, reads, writes)
        return tok

    def wait_tok(self, eng, tok):
        self._wait(eng, tok)

    def alias(self, old, new):
        toks = []
        for b in old:
            if b.w is not None:
                toks.append(b.w)
            toks.extend(b.r)
        best = {}
        for t in toks:
            if t[0] not in best or best[t[0]][2] < t[2]:
                best[t[0]] = t
        for b in new:
            b.w = None
            b.r = list(best.values())

    def emit(self):
        nc = self.nc
        with nc.Block() as block:
            def replay(name):
                def f(eng):
                    for it in self.q[name]:
                        if it[0] == "wait":
                            eng.wait_ge(it[1], it[2])
                        else:
                            ins = it[1](eng)
                            if it[2] is not None:
                                ins.then_inc(it[2], it[3])
                return f
            block.tensor(replay("tensor"))
            block.vector(replay("vector"))
            block.scalar(replay("scalar"))
            block.gpsimd(replay("gpsimd"))
            block.sync(replay("sync"))


def build_program(nl=DEPTH, stop=None):
    nc = bass.Bass("TRN2", target_bir_lowering=False)
    dt_in = lambda name, shape, dt=F32: nc.dram_tensor(name, list(shape), dt, kind="ExternalInput").ap()
    x_in = dt_in("x", [NB * 128, D])
    wsh = {
        "w_in": dt_in("w_in", [DEPTH * D // NC, INC]),
        "w_out": dt_in("w_out", [DEPTH * D // NC, D]),
        "w_up": dt_in("w_up", [DEPTH * D // NC, 2 * FF]),
        "w_down": dt_in("w_down", [DEPTH * FF // NC, D]),
    }
    norm_mix = dt_in("norm_mix", [DEPTH, D])
    norm_ffn = dt_in("norm_ffn", [DEPTH, D])
    norm_final = dt_in("norm_final", [1, D])
    lam_in = dt_in("lam4", [DEPTH, 4 * 64])
    subln = dt_in("diff_subln", [DEPTH, 128])
    sgu_g = dt_in("sgu_ln_g", [DEPTH, 512])
    sgu_bn = dt_in("sgu_ln_b", [DEPTH, 512])
    sgu_wT = dt_in("sgu_wT", [DEPTH, 128, 4 * 128])
    sgu_bT = dt_in("sgu_bT", [DEPTH, 128, 4])
    conv_p = dt_in("conv_p", [DEPTH, 128, 88 * 4])
    ropeA_in = dt_in("ropeA", [128, NB * 2 * 16])
    ropeB_in = dt_in("ropeB", [128, NB * 2 * 8])
    maskA_in = dt_in("maskA", [128, 8 * 3 * 128], BF16)
    maskB_in = dt_in("maskB", [128, 8 * 128], BF16)
    ident_in = dt_in("ident", [128, 128], BF16)
    triu_in = dt_in("triu", [128, 128])
    sel_in = dt_in("sel", [128, 9])
    out = nc.dram_tensor("out", [NB * 128, D], F32, kind="ExternalOutput").ap()

    wrows = {"w_in": D, "w_out": D, "w_up": D, "w_down": FF}
    wcols = {"w_in": INC, "w_out": D, "w_up": 2 * FF, "w_down": D}
    wall = {k: [nc.dram_tensor(f"{k}_all{l}", [wrows[k], wcols[k]], F32).ap() for l in range(DEPTH)] for k in wsh}
    wloc = {k: [nc.dram_tensor(f"{k}_loc{l}", [wrows[k] // NC, wcols[k]], F32).ap() for l in range(DEPTH)] for k in wsh}
    kv_loc = [nc.dram_tensor(f"kv_loc{i}", [12 * 128, KVW], BF16).ap() for i in range(2)]
    kv_all = [nc.dram_tensor(f"kv_all{i}", [NC * 12 * 128, KVW], BF16).ap() for i in range(2)]
    halo_loc = [nc.dram_tensor(f"halo_loc{i}", [128, 256], BF16).ap() for i in range(2)]
    halo_all = [nc.dram_tensor(f"halo_all{i}", [NC * 128, 256], BF16).ap() for i in range(2)]
    bKVL = [Buf("kvl0"), Buf("kvl1")]
    bKVA = [Buf("kva0"), Buf("kva1")]
    bHL = [Buf("hl0"), Buf("hl1")]
    bHA = [Buf("ha0"), Buf("ha1")]

    es = ExitStack()
    with es:
        S = Sched(nc, es)
        sbt = lambda name, shape, dt: es.enter_context(nc.sbuf_tensor(name, list(shape), dt))
        X = sbt("X", [128, NB, D], F32)
        HT = sbt("HT", [128, 16, NB, 130], BF16)
        QT = sbt("QT", [128, 12 * 1024], BF16)
        R1N = 28672
        R1 = sbt("R1", [128, R1N], BF16)
        GT = sbt("GT", [128, D], F32)
        MA = sbt("MA", [128, 8, 3, 128], BF16)
        MB = sbt("MB", [128, 8, 128], BF16)
        RA = sbt("RA", [128, NB, 2, 16], F32)
        RB = sbt("RB", [128, NB, 2, 8], F32)
        IDN = sbt("IDN", [128, 128], BF16)
        TRI = sbt("TRI", [128, 128], F32)
        SEL = sbt("SEL", [128, 9], F32)
        CP = sbt("CP", [128, 88, 4], F32)
        WST = sbt("WST", [128, 4, 128], BF16)
        WSF = sbt("WSF", [128, 4, 128], F32)
        LNG = sbt("LNG", [128, 512], F32)
        LNB = sbt("LNB", [128, 512], F32)
        BST = sbt("BST", [128, 4], F32)
        GSUB = sbt("GSUB", [128, 128], F32)
        LAMV = sbt("LAMV", [128, 256], F32)
        LAMP = sbt("LAMP", [128, 128], F32)
        SM = sbt("SM", [128, 64], F32)
        EPST = sbt("EPST", [128, 1], F32)
        RT = sbt("RT", [128, 4, 96], F32)

        PS = [es.enter_context(nc.psum_tensor(f"PS{i}", [128, 512], F32)) for i in range(8)]
        BPS = [Buf(f"PS{i}") for i in range(8)]
        psb = lambda i: PS[i][:].bitcast(BF16)

        B = {n: Buf(n) for n in "HT QT GT MA MB RA RB IDN TRI SEL CP WST WSF LNG LNB BST GSUB LAMV LAMP EPST RT".split()}
        BXs = [Buf(f"X{s}") for s in range(NB)]
        BSM = {}

        def smb(i):
            if i not in BSM:
                BSM[i] = Buf(f"SM{i}")
            return BSM[i]

        def r1(off, n, dt=BF16):
            assert off + n <= R1N and off % 2 == 0
            ap = R1[:, off:off + n]
            return ap.bitcast(F32) if dt == F32 else ap

        state = {"r1prev": [], "ev": 0}

        def phase_bufs(names):
            new = [Buf(n) for n in names]
            S.alias(state["r1prev"], new)
            state["r1prev"] = new
            return new

        def ev_eng():
            state["ev"] += 1
            return "vector" if state["ev"] % 2 else "scalar"

        def evac(eng, out_ap, in_ap, reads, writes):
            if eng == "vector":
                return S.op("vector", lambda e: e.tensor_copy(out=out_ap, in_=in_ap), reads=reads, writes=writes)
            return S.op("scalar", lambda e: e.activation(out=out_ap, in_=in_ap, func=AF.Copy), reads=reads, writes=writes)

        def ld(dst, src, b, q="sync"):
            S.dma(q, "c_" + b.name, lambda e: e.dma_start(out=dst, in_=src), writes=[b])

        for s in range(NB):
            S.dma("sync", f"x{s}", lambda e, s=s: e.dma_start(out=X[:, s, :], in_=x_in[s * 128:(s + 1) * 128, :]), writes=[BXs[s]])
        ld(MA[:], maskA_in.rearrange("p (j d q) -> p j d q", j=8, d=3), B["MA"])
        ld(MB[:], maskB_in.rearrange("p (j q) -> p j q", j=8), B["MB"])
        ld(RA[:], ropeA_in.rearrange("p (s t d) -> p s t d", s=NB, t=2), B["RA"])
        ld(RB[:], ropeB_in.rearrange("p (s t d) -> p s t d", s=NB, t=2), B["RB"])
        ld(IDN[:], ident_in, B["IDN"])
        ld(TRI[:], triu_in, B["TRI"])
        ld(SEL[:], sel_in, B["SEL"])
        S.op("vector", lambda e: e.memset(EPST[:], EPS), writes=[B["EPST"]])

        BW = {k: [Buf(f"W_{k}{l}") for l in range(DEPTH)] for k in wall}
        BWL = {k: [Buf(f"WL_{k}{l}") for l in range(DEPTH)] for k in wall}
        WKEYS = ("w_in", "w_out", "w_up", "w_down")
        for l in range(nl):
            for k in WKEYS:
                rl = wrows[k] // NC
                S.dma("sync", f"wcp_{k}{l}", lambda e, k=k, l=l, rl=rl: e.dma_start(out=wloc[k][l], in_=wsh[k][l * rl:(l + 1) * rl, :]), writes=[BWL[k][l]])

        def issue_wag(l):
            for k in WKEYS:
                S.dma("gpsimd", "ccw_" + k, lambda e, k=k, l=l: e.collective_compute(
                    "AllGather", ALU.bypass, replica_groups=[list(range(NC))], ins=[wloc[k][l]], outs=[wall[k][l]]),
                    reads=[BWL[k][l]], writes=[BW[k][l]], inc=1)

        issue_wag(0)

        def rstd_from_ss(ss_ap, rstd_ap, n, bss, brstd):
            S.op("scalar", lambda e: e.activation(out=rstd_ap, in_=ss_ap, func=AF.Ln, bias=EPST[:, 0:1], scale=1.0 / n),
                 reads=[bss, B["EPST"]], writes=[brstd])
            S.op("scalar", lambda e: e.activation(out=rstd_ap, in_=rstd_ap, func=AF.Exp, scale=-0.5),
                 reads=[brstd], writes=[brstd])

        def rmsnorm_to_HT(gain_row, psbanks):
            bhb0, bhb1, bjunk, B["HALO"], B["HSEL"], B["HLS"] = phase_bufs(["hb0", "hb1", "junk", "HALO", "HSEL", "HLS"])
            hbt = [r1(0, 2048), r1(2048, 2048)]
            hbb = [bhb0, bhb1]
            junk = r1(4096, 2048)
            S.dma("sync", "gt", lambda e: e.dma_start(out=GT[:], in_=gain_row.broadcast_to([128, D])), writes=[B["GT"]])
            for s in range(NB):
                ss = SM[:, s:s + 1]
                rs = SM[:, 8 + s:9 + s]
                S.op("scalar", lambda e, s=s, ss=ss: e.activation(out=junk, in_=X[:, s, :], func=AF.Square, accum_out=ss),
                     reads=[BXs[s]], writes=[bjunk, smb(s)])
                rstd_from_ss(ss, rs, D, smb(s), smb(8 + s))
                hb = hbt[s % 2]
                S.op("vector", lambda e, s=s, rs=rs, hb=hb: e.scalar_tensor_tensor(out=hb, in0=X[:, s, :], scalar=rs, in1=GT[:], op0=ALU.mult, op1=ALU.mult),
                     reads=[BXs[s], smb(8 + s), B["GT"]], writes=[hbb[s % 2]])
                for half in range(2):
                    pb = psbanks[half]
                    for kk in range(8):
                        k = half * 8 + kk
                        S.op("tensor", lambda e, k=k, kk=kk, pb=pb, hb=hb: e.transpose(out=psb(pb)[:, kk * 128:(kk + 1) * 128], in_=hb[:, k * 128:(k + 1) * 128], identity=IDN[:]),
                             reads=[hbb[s % 2], B["IDN"]], writes=[BPS[pb]], signal=(kk == 7))
                    evac(ev_eng(), HT[:, half * 8:(half + 1) * 8, s, 2:130], psb(pb).rearrange("p (k t) -> p k t", k=8),
                         reads=[BPS[pb]], writes=[B["HT"]])

        HALO = r1(24576, 2048).rearrange("p (r c) -> p r c", r=NC)
        HSEL = r1(24576 + 2048, 512, F32)
        HLS = r1(24576 + 2560, 256)

        def dump_X():
            toks = []
            for s in range(NB):
                toks.append(S.dma("sync", "dbg", lambda e, s=s: e.dma_start(out=out[s * 128:(s + 1) * 128, :], in_=X[:, s, :]), reads=[BXs[s]], writes=[bOut]))
            S.wait_tok("sync", toks[-1])
            S.emit()

        def dump_HT():
            (bYd,) = phase_bufs(["Yd"])
            Yd = r1(0, 4096, F32)
            toks = []
            for s in range(NB):
                S.op("vector", lambda e, s=s: e.tensor_copy(out=Yd.rearrange("p (k t) -> p k t", k=16), in_=HT[:, :, s, 2:130]), reads=[B["HT"]], writes=[bYd])
                toks.append(S.dma("sync", "dbg", lambda e, s=s: e.dma_start(out=out[s * 128:(s + 1) * 128, :], in_=Yd), reads=[bYd], writes=[bOut]))
            S.wait_tok("sync", toks[-1])
            S.emit()

        bOut = Buf("out")

        for l in range(nl):
            par = l % 2
            lambda_init = 0.8 - 0.6 * math.exp(-0.3 * l)
            ld(LAMV[:], lam_in[l:l + 1, :].broadcast_to([128, 256]), B["LAMV"])
            ld(GSUB[:], subln[l:l + 1, :].broadcast_to([128, 128]), B["GSUB"])
            ld(LNG[:], sgu_g[l:l + 1, :].broadcast_to([128, 512]), B["LNG"])
            ld(LNB[:], sgu_bn[l:l + 1, :].broadcast_to([128, 512]), B["LNB"])
            ld(WSF[:], sgu_wT[l].rearrange("p (g i) -> p g i", g=4), B["WSF"])
            ld(BST[:], sgu_bT[l], B["BST"])
            ld(CP[:], conv_p[l].rearrange("p (c t) -> p c t", t=4), B["CP"])
            S.op("vector", lambda e: e.tensor_tensor(out=LAMP[:, 0:64], in0=LAMV[:, 0:64], in1=LAMV[:, 64:128], op=ALU.mult), reads=[B["LAMV"]], writes=[B["LAMP"]])
            S.op("vector", lambda e: e.tensor_tensor(out=LAMP[:, 64:128], in0=LAMV[:, 128:192], in1=LAMV[:, 192:256], op=ALU.mult), reads=[B["LAMV"]], writes=[B["LAMP"]])
            S.op("scalar", lambda e: e.activation(out=LAMV[:, 0:64], in_=LAMP[:, 0:64], func=AF.Copy, accum_out=SM[:, 16:17]), reads=[B["LAMP"]], writes=[B["LAMV"], smb(16)])
            S.op("scalar", lambda e: e.activation(out=LAMV[:, 64:128], in_=LAMP[:, 64:128], func=AF.Copy, accum_out=SM[:, 17:18]), reads=[B["LAMP"]], writes=[B["LAMV"], smb(17)])
            S.op("scalar", lambda e: e.activation(out=SM[:, 16:18], in_=SM[:, 16:18], func=AF.Exp), reads=[smb(16), smb(17)], writes=[smb(16), smb(17)])
            S.op("vector", lambda e: e.tensor_tensor(out=SM[:, 18:19], in0=SM[:, 16:17], in1=SM[:, 17:18], op=ALU.subtract), reads=[smb(16), smb(17)], writes=[smb(18)])
            S.op("vector", lambda e: e.tensor_scalar(out=SM[:, 19:20], in0=SM[:, 18:19], scalar1=lambda_init, scalar2=-1.0, op0=ALU.add, op1=ALU.mult), reads=[smb(18)], writes=[smb(19)])
            S.op("vector", lambda e: e.tensor_scalar(out=GSUB[:], in0=GSUB[:], scalar1=1.0 - lambda_init, scalar2=None, op0=ALU.mult), reads=[B["GSUB"]], writes=[B["GSUB"]])
            S.op("vector", lambda e: e.tensor_tensor(out=WST[:], in0=WSF[:], in1=TRI[:].unsqueeze(1).broadcast_to([128, 4, 128]), op=ALU.mult), reads=[B["WSF"], B["TRI"]], writes=[B["WST"]])

            rmsnorm_to_HT(norm_mix[l:l + 1, :], (4, 5))

            bST, bW0, bW1, bVKS, bSTB0, bSTB1 = phase_bufs(["ST", "Win0", "Win1", "VKS", "STB0", "STB1"])
            ST = r1(0, 12288, F32).rearrange("p (s c) -> p s c", s=NB)
            WIN = [r1(12288 + i * 4096, 4096).rearrange("p (k c) -> p k c", k=16) for i in range(2)]
            bWIN = [bW0, bW1]
            VKO = 12288 + 8192
            VST = r1(VKO, 6 * NB * VW).rearrange("p (h s c) -> p h s c", h=6, s=NB)
            KTS = r1(VKO, 6 * 1024).rearrange("p (h n) -> p h n", h=6)
            GU = r1(VKO, NB * 512).rearrange("p (s c) -> p s c", s=NB)
            STB = [r1(VKO + 6336 + i * 768, 768) for i in range(2)]
            bSTB = [bSTB0, bSTB1]
            QTv = QT[:].rearrange("p (h n) -> p h n", h=12)

            win_view = wall["w_in"][l].rearrange("(k p) c -> p k c", p=128)
            gcount = [0]

            def inproj_group(c0, gw, consume):
                gi = gcount[0]
                gcount[0] += 1
                wb = gi % 2
                S.dma("gpsimd", f"win{wb}", lambda e: e.dma_start(out=WIN[wb][:, :, 0:gw], in_=win_view[:, :, c0:c0 + gw]),
                      reads=[BW["w_in"][l]], writes=[bWIN[wb]])
                for s in range(NB):
                    pb = (gi * NB + s) % 4
                    for k in range(16):
                        S.op("tensor", lambda e, k=k, s=s, pb=pb: e.matmul(PS[pb][:, 0:gw], lhsT=HT[:, k, s, 2:130], rhs=WIN[wb][:, k, 0:gw], start=(k == 0), stop=(k == 15)),
                             reads=[B["HT"], bWIN[wb]], writes=[BPS[pb]], signal=(k == 15))
                    consume(s, PS[pb][:, 0:gw], BPS[pb])

            kvl = kv_loc[par].rearrange("(h p) c -> p h c", p=128)

            for (c_base, h0) in ((1536, 0), (3840, 6)):
                S.op("gpsimd", lambda e: e.memset(VST, 1.0), writes=[bVKS])
                for gi in range(3):
                    def cons(s, ps, bps, gi=gi):
                        evac(ev_eng(), VST[:, gi * 2:gi * 2 + 2, s, 0:128], ps.rearrange("p (h d) -> p h d", h=2), reads=[bps], writes=[bVKS])
                    inproj_group(c_base + gi * 256, 256, cons)
                S.dma("sync", "vout", lambda e, h0=h0: e.dma_start(out=kvl[:, h0:h0 + 6, 1024:KVW], in_=VST.rearrange("p h s c -> p h (s c)")),
                      reads=[bVKS], writes=[bKVL[par]])

            def rope(s, kindB):
                nh, hd, TB_ = (12, 8, RB) if kindB else (6, 16, RA)
                bt = B["RB"] if kindB else B["RA"]
                v = ST[:, s, :].rearrange("p (h d) -> p h d", h=nh)
                x1 = v[:, :, 0:hd]
                x2 = v[:, :, hd:2 * hd]
                cs = TB_[:, s, 0:1, :].broadcast_to([128, nh, hd])
                sn = TB_[:, s, 1:2, :].broadcast_to([128, nh, hd])
                t = [RT[:, i, :].rearrange("p (h d) -> p h d", h=nh) for i in range(4)]
                rd = [bST, bt]
                S.op("vector", lambda e: e.tensor_tensor(out=t[0], in0=x1, in1=cs, op=ALU.mult), reads=rd, writes=[B["RT"]])
                S.op("vector", lambda e: e.tensor_tensor(out=t[1], in0=x2, in1=sn, op=ALU.mult), reads=rd, writes=[B["RT"]])
                S.op("vector", lambda e: e.tensor_tensor(out=t[2], in0=x1, in1=sn, op=ALU.mult), reads=rd, writes=[B["RT"]])
                S.op("vector", lambda e: e.tensor_tensor(out=t[3], in0=x2, in1=cs, op=ALU.mult), reads=rd, writes=[B["RT"]])
                S.op("vector", lambda e: e.tensor_tensor(out=x1, in0=t[0], in1=t[1], op=ALU.subtract), reads=[B["RT"]], writes=[bST])
                S.op("vector", lambda e: e.tensor_tensor(out=x2, in0=t[3], in1=t[2], op=ALU.add), reads=[B["RT"]], writes=[bST])

            def qk_piece(c_base, kindB, dst_fn, dst_buf):
                for gi in range(3):
                    def cons(s, ps, bps, gi=gi):
                        evac(ev_eng(), ST[:, s, gi * 256:(gi + 1) * 256], ps, reads=[bps], writes=[bST])
                    inproj_group(c_base + gi * 256, 256, cons)
                for s in range(NB):
                    rope(s, kindB)
                    sb_ = STB[s % 2]
                    S.op("gpsimd", lambda e, s=s, sb_=sb_: e.tensor_copy(out=sb_, in_=ST[:, s, :]), reads=[bST], writes=[bSTB[s % 2]])
                    for h in range(6):
                        S.op("tensor", lambda e, h=h, sb_=sb_: e.transpose(out=psb(6)[:, h * 128:(h + 1) * 128], in_=sb_[:, h * 128:(h + 1) * 128], identity=IDN[:]),
                             reads=[bSTB[s % 2], B["IDN"]], writes=[BPS[6]], signal=(h == 5))
                    evac(ev_eng(), dst_fn(s), psb(6)[:, 0:768].rearrange("p (h t) -> p h t", h=6), reads=[BPS[6]], writes=[dst_buf])

            for (c_base, h0, kindB) in ((768, 0, False), (3072, 6, True)):
                qk_piece(c_base, kindB, lambda s: KTS[:, :, s * 128:(s + 1) * 128], bVKS)
                S.dma("sync", "kout", lambda e, h0=h0: e.dma_start(out=kvl[:, h0:h0 + 6, 0:1024], in_=KTS),
                      reads=[bVKS], writes=[bKVL[par]])
            t = S.dma("gpsimd", "cc", lambda e: e.collective_compute(
                "AllGather", ALU.bypass, replica_groups=[list(range(NC))], ins=[kv_loc[par]], outs=[kv_all[par]]),
                reads=[bKVL[par]], writes=[bKVA[par]], inc=1)
            S.wait_tok("gpsimd", t)
            if l + 1 < nl:
                issue_wag(l + 1)
            for (c_base, h0, kindB) in ((0, 0, False), (2304, 6, True)):
                qk_piece(c_base, kindB, lambda s, h0=h0: QTv[:, h0:h0 + 6, s * 128:(s + 1) * 128], B["QT"])
            for gi in range(2):
                def cons(s, ps, bps, gi=gi):
                    S.op("scalar", lambda e: e.activation(out=GU[:, s, gi * 256:(gi + 1) * 256], in_=ps, func=AF.Gelu), reads=[bps], writes=[bVKS])
                inproj_group(4608 + gi * 256, 256, cons)
            for gi in range(2):
                def cons(s, ps, bps, gi=gi):
                    S.op("scalar", lambda e: e.activation(out=ST[:, s, gi * 256:(gi + 1) * 256], in_=ps, func=AF.Gelu), reads=[bps], writes=[bST])
                inproj_group(5120 + gi * 256, 256, cons)
            for s in range(NB):
                gv = ST[:, s, 0:512]
                sc = ST[:, s, 512:768]
                jk = STB[0][:, 0:512]
                S.op("scalar", lambda e: e.activation(out=jk, in_=gv, func=AF.Copy, accum_out=SM[:, 24:25]), reads=[bST], writes=[bSTB[0], smb(24)])
                S.op("scalar", lambda e: e.activation(out=jk, in_=gv, func=AF.Square, accum_out=SM[:, 25:26]), reads=[bST], writes=[bSTB[0], smb(25)])
                S.op("vector", lambda e: e.tensor_scalar(out=SM[:, 26:27], in0=SM[:, 24:25], scalar1=1.0 / 512, scalar2=None, op0=ALU.mult), reads=[smb(24)], writes=[smb(26)])
                S.op("vector", lambda e: e.tensor_tensor(out=SM[:, 27:28], in0=SM[:, 26:27], in1=SM[:, 26:27], op=ALU.mult), reads=[smb(26)], writes=[smb(27)])
                S.op("vector", lambda e: e.scalar_tensor_tensor(out=SM[:, 28:29], in0=SM[:, 25:26], scalar=1.0 / 512, in1=SM[:, 27:28], op0=ALU.mult, op1=ALU.subtract), reads=[smb(25), smb(27)], writes=[smb(28)])
                rstd_from_ss(SM[:, 28:29], SM[:, 28:29], 1.0, smb(28), smb(28))
                S.op("vector", lambda e: e.tensor_scalar(out=gv, in0=gv, scalar1=SM[:, 26:27], scalar2=SM[:, 28:29], op0=ALU.subtract, op1=ALU.mult), reads=[bST, smb(26), smb(28)], writes=[bST])
                S.op("vector", lambda e: e.tensor_tensor(out=gv, in0=gv, in1=LNG[:], op=ALU.mult), reads=[bST, B["LNG"]], writes=[bST])
                vn = STB[0][:, 0:512]
                S.op("vector", lambda e: e.tensor_tensor(out=vn, in0=gv, in1=LNB[:], op=ALU.add), reads=[bST, B["LNB"]], writes=[bSTB[0]])
                for g in range(4):
                    S.op("tensor", lambda e, g=g: e.matmul(PS[0][:, g * 128:(g + 1) * 128], lhsT=WST[:, g, :], rhs=vn[:, g * 128:(g + 1) * 128], start=True, stop=True),
                         reads=[bSTB[0], B["WST"]], writes=[BPS[0]], signal=(g == 3))
                oc = STB[1][:, 0:512]
                for g in range(4):
                    S.op("vector", lambda e, g=g: e.scalar_tensor_tensor(out=oc[:, g * 128:(g + 1) * 128], in0=PS[0][:, g * 128:(g + 1) * 128], scalar=BST[:, g:g + 1], in1=GU[:, s, g * 128:(g + 1) * 128], op0=ALU.add, op1=ALU.mult),
                         reads=[BPS[0], B["BST"], bVKS], writes=[bSTB[1]])
                for g in range(4):
                    S.op("tensor", lambda e, g=g: e.transpose(out=psb(6)[:, g * 128:(g + 1) * 128], in_=oc[:, g * 128:(g + 1) * 128], identity=IDN[:]),
                         reads=[bSTB[1], B["IDN"]], writes=[BPS[6]], signal=(g == 3))
                evac(ev_eng(), HT[:, 12:16, s, 2:130], psb(6)[:, 0:512].rearrange("p (k t) -> p k t", k=4), reads=[BPS[6]], writes=[B["HT"]])

            if stop == "h":
                dump_HT()
                return nc
            names = ["KV0", "KV1", "PT0", "PT1", "OT0", "OT1", "OB", "TBf", "JB"]
            bKV0, bKV1, bPT0, bPT1, bOT0, bOT1, bOB, bTB, bJB = phase_bufs(names)
            KVH = 4 * KVW
            KV = [r1(i * KVH, KVH).rearrange("p (j c) -> p j c", j=4) for i in range(2)]
            bKV = [bKV0, bKV1]
            PT = [r1(2 * KVH + i * 1024, 1024).rearrange("p (m n) -> p m n", m=2) for i in range(2)]
            bPT = [bPT0, bPT1]
            o2 = 2 * KVH + 2048
            OT = [r1(o2 + i * 128, 128) for i in range(2)]
            bOT = [bOT0, bOT1]
            OB = r1(o2 + 256, 256, F32)
            TBf = r1(o2 + 512, 256, F32)
            JB = r1(o2 + 768, 128)
            kva = kv_all[par].rearrange("(j h p) c -> p h j c", j=NC, h=12, p=128)
            cnt = {"kv": 0, "pt": 0, "ot": 0}
            bOacc = [Buf(f"Oacc{a}") for a in range(8)]
            S.alias([BPS[4], BPS[5], BPS[6]], bOacc)

            for hh in range(12):
                isB = hh >= 6
                nm = 2 if isB else 1
                scale = (64 ** -0.5) if isB else (128 ** -0.5)
                rows = [(0, 64), (64, 128)] if isB else [(0, 128)]
                for g in range(2):
                    s0 = 4 * g
                    sig_lo = 0 if isB else max(0, s0 - 2)
                    sig_hi = s0 + 3
                    first_sig = {s: (0 if isB else max(0, s - 2)) for s in range(s0, s0 + 4)}
                    started = set()
                    for half in range(2):
                        kvb = cnt["kv"] % 2
                        cnt["kv"] += 1
                        c_lo, c_hi = sig_lo * 128, (sig_hi + 1) * 128
                        v_lo, v_hi = 1024 + sig_lo * VW, 1024 + (sig_hi + 1) * VW
                        S.dma("sync", f"kv{kvb}", lambda e, kvb=kvb, half=half, c_lo=c_lo, c_hi=c_hi: e.dma_start(
                            out=KV[kvb][:, :, c_lo:c_hi], in_=kva[:, hh, half * 4:half * 4 + 4, c_lo:c_hi]), reads=[bKVA[par]], writes=[bKV[kvb]])
                        S.dma("sync", f"kv{kvb}", lambda e, kvb=kvb, half=half, v_lo=v_lo, v_hi=v_hi: e.dma_start(
                            out=KV[kvb][:, :, v_lo:v_hi], in_=kva[:, hh, half * 4:half * 4 + 4, v_lo:v_hi]), reads=[bKVA[par]], writes=[bKV[kvb]])
                        for sig in range(sig_lo, sig_hi + 1):
                            if isB:
                                sa, sb2 = max(s0, sig), s0 + 3
                            else:
                                sa, sb2 = max(s0, sig), min(s0 + 3, sig + 2)
                            if sa > sb2:
                                continue
                            n = (sb2 - sa + 1) * 128
                            for jj in range(4):
                                j = half * 4 + jj
                                ptb = cnt["pt"] % 2
                                cnt["pt"] += 1
                                for m in range(nm):
                                    r0, r1_ = rows[m]
                                    pbk = 2 * ptb + m
                                    S.op("tensor", lambda e, m=m, r0=r0, r1_=r1_, pbk=pbk, jj=jj, sig=sig, sa=sa, n=n, kvb=kvb: e.matmul(
                                        PS[pbk][:, 0:n], lhsT=KV[kvb][r0:r1_, jj, sig * 128:(sig + 1) * 128],
                                        rhs=QTv[r0:r1_, hh, sa * 128:sa * 128 + n], start=True, stop=True),
                                        reads=[bKV[kvb], B["QT"]], writes=[BPS[pbk]])
                                    S.op("scalar", lambda e, m=m, pbk=pbk, ptb=ptb, n=n: e.activation(out=PT[ptb][:, m, 0:n], in_=PS[pbk][:, 0:n], func=AF.Exp, scale=scale),
                                         reads=[BPS[pbk]], writes=[bPT[ptb]])
                                if isB:
                                    if sa == sig:
                                        for m in range(2):
                                            S.op("vector", lambda e, ptb=ptb, j=j, m=m: e.tensor_tensor(out=PT[ptb][:, m, 0:128], in0=PT[ptb][:, m, 0:128],
                                                 in1=MB[:, j, :], op=ALU.mult), reads=[bPT[ptb], B["MB"]], writes=[bPT[ptb]])
                                else:
                                    nn = sb2 - sa + 1
                                    for si in range(nn):
                                        S.op("vector", lambda e, ptb=ptb, j=j, sa=sa, sig=sig, si=si: e.tensor_tensor(
                                            out=PT[ptb][:, 0, si * 128:(si + 1) * 128], in0=PT[ptb][:, 0, si * 128:(si + 1) * 128],
                                            in1=MA[:, j, sa - sig + si, :], op=ALU.mult), reads=[bPT[ptb], B["MA"]], writes=[bPT[ptb]])
                                for s in range(sa, sb2 + 1):
                                    first = (half == 0 and sig == first_sig[s] and jj == 0)
                                    last = (half == 1 and sig == s and jj == 3)
                                    for m in range(nm):
                                        a = (s - s0) * nm + m
                                        pbo = 4 + a // 3
                                        co = (a % 3) * 129
                                        first = pbo not in started
                                        started.add(pbo)
                                        S.op("tensor", lambda e, m=m, s=s, pbo=pbo, co=co, ptb=ptb, kvb=kvb, jj=jj, sig=sig, first=first, last=last, sa=sa: e.matmul(
                                            PS[pbo][:, co:co + 129], lhsT=PT[ptb][:, m, (s - sa) * 128:(s - sa + 1) * 128],
                                            rhs=KV[kvb][:, jj, 1024 + sig * VW:1024 + sig * VW + 129], start=first, stop=last),
                                            reads=[bPT[ptb], bKV[kvb]], writes=[bOacc[a]], signal=last)
                                    if last:
                                        otb = cnt["ot"] % 2
                                        cnt["ot"] += 1
                                        a0 = (s - s0) * nm
                                        pb0, c0_ = 4 + a0 // 3, (a0 % 3) * 129
                                        if not isB:
                                            S.op("vector", lambda e, pb0=pb0, c0_=c0_: e.reciprocal(out=SM[:, 32:33], in_=PS[pb0][:, c0_ + 128:c0_ + 129]), reads=[bOacc[a0]], writes=[smb(32)])
                                            S.op("vector", lambda e, pb0=pb0, c0_=c0_, otb=otb: e.tensor_scalar(out=OT[otb], in0=PS[pb0][:, c0_:c0_ + 128], scalar1=SM[:, 32:33], scalar2=None, op0=ALU.mult),
                                                 reads=[bOacc[a0], smb(32)], writes=[bOT[otb]])
                                        else:
                                            a1 = a0 + 1
                                            pb1, c1_ = 4 + a1 // 3, (a1 % 3) * 129
                                            S.op("vector", lambda e, pb0=pb0, c0_=c0_: e.reciprocal(out=SM[:, 32:33], in_=PS[pb0][:, c0_ + 128:c0_ + 129]), reads=[bOacc[a0]], writes=[smb(32)])
                                            S.op("vector", lambda e, pb1=pb1, c1_=c1_: e.reciprocal(out=SM[:, 33:34], in_=PS[pb1][:, c1_ + 128:c1_ + 129]), reads=[bOacc[a1]], writes=[smb(33)])
                                            S.op("vector", lambda e: e.tensor_tensor(out=SM[:, 33:34], in0=SM[:, 33:34], in1=SM[:, 19:20], op=ALU.mult), reads=[smb(33), smb(19)], writes=[smb(33)])
                                            S.op("vector", lambda e, pb1=pb1, c1_=c1_: e.tensor_scalar(out=TBf, in0=PS[pb1][:, c1_:c1_ + 128], scalar1=SM[:, 33:34], scalar2=None, op0=ALU.mult),
                                                 reads=[bOacc[a1], smb(33)], writes=[bTB])
                                            S.op("vector", lambda e, pb0=pb0, c0_=c0_: e.scalar_tensor_tensor(out=OB, in0=PS[pb0][:, c0_:c0_ + 128], scalar=SM[:, 32:33], in1=TBf, op0=ALU.mult, op1=ALU.add),
                                                 reads=[bOacc[a0], smb(32), bTB], writes=[bOB])
                                            S.op("scalar", lambda e: e.activation(out=JB, in_=OB, func=AF.Square, accum_out=SM[:, 34:35]), reads=[bOB], writes=[bJB, smb(34)])
                                            rstd_from_ss(SM[:, 34:35], SM[:, 35:36], 128, smb(34), smb(35))
                                            S.op("vector", lambda e, otb=otb: e.scalar_tensor_tensor(out=OT[otb], in0=OB, scalar=SM[:, 35:36], in1=GSUB[:], op0=ALU.mult, op1=ALU.mult),
                                                 reads=[bOB, smb(35), B["GSUB"]], writes=[bOT[otb]])
                                        S.op("tensor", lambda e, otb=otb: e.transpose(out=psb(7)[:, otb * 128:(otb + 1) * 128], in_=OT[otb], identity=IDN[:]),
                                             reads=[bOT[otb], B["IDN"]], writes=[BPS[7]])
                                        evac(ev_eng(), HT[:, hh, s, 2:130], psb(7)[:, otb * 128:(otb + 1) * 128], reads=[BPS[7]], writes=[B["HT"]])

            S.alias(bOacc + [BPS[4], BPS[5], BPS[6]], [BPS[4], BPS[5], BPS[6]])
            if stop == "mix":
                dump_HT()
                return nc
            bWo0, bWo1 = phase_bufs(["Wo0", "Wo1"])
            WO = [r1(i * 8192, 8192).rearrange("p (k c) -> p k c", k=16) for i in range(2)]
            bWO = [bWo0, bWo1]
            wo_view = wall["w_out"][l].rearrange("(k p) c -> p k c", p=128)
            for cg in range(4):
                wb = cg % 2
                S.dma("gpsimd", f"wo{wb}", lambda e, wb=wb, cg=cg: e.dma_start(out=WO[wb], in_=wo_view[:, :, cg * 512:(cg + 1) * 512]),
                      reads=[BW["w_out"][l]], writes=[bWO[wb]])
                for s in range(NB):
                    pb = (cg * NB + s) % 4
                    for k in range(16):
                        S.op("tensor", lambda e, k=k, s=s, pb=pb, wb=wb: e.matmul(PS[pb][:, :], lhsT=HT[:, k, s, 2:130], rhs=WO[wb][:, k, :], start=(k == 0), stop=(k == 15)),
                             reads=[B["HT"], bWO[wb]], writes=[BPS[pb]], signal=(k == 15))
                    S.op("vector", lambda e, s=s, pb=pb, cg=cg: e.tensor_tensor(out=X[:, s, cg * 512:(cg + 1) * 512], in0=X[:, s, cg * 512:(cg + 1) * 512], in1=PS[pb][:, :], op=ALU.add),
                         reads=[BPS[pb], BXs[s]], writes=[BXs[s]])

            if stop == "xm":
                dump_X()
                return nc
            rmsnorm_to_HT(norm_ffn[l:l + 1, :], (4, 5))
            S.op("vector", lambda e: e.tensor_copy(out=HLS.rearrange("p (k s t) -> p k s t", k=16, s=NB), in_=HT[:, :, :, 128:130]), reads=[B["HT"]], writes=[B["HLS"]])
            S.dma("sync", "hlo", lambda e: e.dma_start(out=halo_loc[par], in_=HLS), reads=[B["HLS"]], writes=[bHL[par]])
            t = S.dma("gpsimd", "cc", lambda e: e.collective_compute(
                "AllGather", ALU.bypass, replica_groups=[list(range(NC))], ins=[halo_loc[par]], outs=[halo_all[par]]),
                reads=[bHL[par]], writes=[bHA[par]], inc=1)
            S.wait_tok("gpsimd", t)
            S.dma("sync", "hli", lambda e: e.dma_start(out=HALO, in_=halo_all[par].rearrange("(r p) c -> p r c", p=128)), reads=[bHA[par]], writes=[B["HALO"]])
            S.op("vector", lambda e: e.tensor_scalar(out=HSEL, in0=HALO[:, 0, :], scalar1=SEL[:, 0:1], scalar2=None, op0=ALU.mult), reads=[B["HALO"], B["SEL"]], writes=[B["HSEL"]])
            for r in range(1, NC):
                S.op("vector", lambda e, r=r: e.scalar_tensor_tensor(out=HSEL, in0=HALO[:, r, :], scalar=SEL[:, r:r + 1], in1=HSEL, op0=ALU.mult, op1=ALU.add),
                     reads=[B["HALO"], B["SEL"], B["HSEL"]], writes=[B["HSEL"]])
            hs4 = HSEL.rearrange("p (k s t) -> p k s t", k=16, s=NB)
            h74 = HALO[:, 7, :].rearrange("p (k s t) -> p k s t", k=16, s=NB)
            S.op("vector", lambda e: e.scalar_tensor_tensor(out=hs4[:, :, 1:NB, :], in0=h74[:, :, 0:NB - 1, :], scalar=SEL[:, 8:9], in1=hs4[:, :, 1:NB, :], op0=ALU.mult, op1=ALU.add),
                 reads=[B["HALO"], B["SEL"], B["HSEL"]], writes=[B["HSEL"]])
            S.op("vector", lambda e: e.tensor_copy(out=HT[:, :, :, 0:2], in_=hs4), reads=[B["HSEL"]], writes=[B["HT"]])

            names = ["Wu0", "Wu1", "Wd0", "Wd1"] + [f"ACC{i}" for i in range(4)] + ["SG0", "SG1"]
            pb_ = phase_bufs(names)
            bWU, bWD, bACC, bSG = pb_[0:2], pb_[2:4], pb_[4:8], pb_[8:10]
            WU = [r1(i * 4096, 4096).rearrange("p (k c) -> p k c", k=16) for i in range(2)]
            WD = [r1(8192 + i * 5632, 5632).rearrange("p (k c) -> p k c", k=11) for i in range(2)]
            ao = 8192 + 2 * 5632
            ACC = [r1(ao + i * 780, 780, F32) for i in range(4)]
            SG = [r1(ao + 4 * 780 + i * 780, 780, F32) for i in range(2)]
            GQ = QT[:, 0:11 * 1024].rearrange("p (i n) -> p i n", i=11)
            wu_view = wall["w_up"][l].rearrange("(k p) c -> p k c", p=128)
            wd_view = wall["w_down"][l].rearrange("(c p) n -> p c n", p=128)
            tranges = [(0, 3), (3, 6), (6, 8)]
            cn = {"u": 0, "acc": 0, "ps": 0, "d": 0, "sg": 0, "pd": 0}
            for qf in range(4):
                for i in range(11):
                    fc = qf * 11 + i
                    wb = cn["u"] % 2
                    cn["u"] += 1
                    S.dma("gpsimd", f"wu{wb}", lambda e, wb=wb, fc=fc: e.dma_start(out=WU[wb][:, :, 0:128], in_=wu_view[:, :, fc * 128:(fc + 1) * 128]),
                          reads=[BW["w_up"][l]], writes=[bWU[wb]])
                    S.dma("gpsimd", f"wu{wb}", lambda e, wb=wb, fc=fc: e.dma_start(out=WU[wb][:, :, 128:256], in_=wu_view[:, :, FF + fc * 128:FF + (fc + 1) * 128]),
                          reads=[BW["w_up"][l]], writes=[bWU[wb]])
                    for (ta, tb) in tranges:
                        nb_ = tb - ta
                        ncol = nb_ * 130
                        accs = []
                        for gv_ in range(2):
                            pb = cn["ps"] % 6
                            cn["ps"] += 1
                            for k in range(16):
                                S.op("tensor", lambda e, k=k, pb=pb, wb=wb, gv_=gv_, ta=ta, tb=tb, ncol=ncol: e.matmul(
                                    PS[pb][:, 0:ncol], lhsT=WU[wb][:, k, gv_ * 128:(gv_ + 1) * 128],
                                    rhs=HT[:, k, ta:tb, :].rearrange("p s t -> p (s t)"), start=(k == 0), stop=(k == 15)),
                                    reads=[B["HT"], bWU[wb]], writes=[BPS[pb]], signal=(k == 15))
                            ai = cn["acc"] % 4
                            cn["acc"] += 1
                            ch = fc if gv_ == 0 else 44 + fc
                            p3 = PS[pb][:, 0:ncol].rearrange("p (s t) -> p s t", t=130)
                            a3 = ACC[ai][:, 0:nb_ * 128].rearrange("p (s t) -> p s t", t=128)
                            S.op("scalar", lambda e, p3=p3, a3=a3, ch=ch: e.activation(out=a3, in_=p3[:, :, 2:130], func=AF.Identity, bias=CP[:, ch, 3:4], scale=CP[:, ch, 2:3]),
                                 reads=[BPS[pb], B["CP"]], writes=[bACC[ai]])
                            S.op("vector", lambda e, p3=p3, a3=a3, ch=ch: e.scalar_tensor_tensor(out=a3, in0=p3[:, :, 1:129], scalar=CP[:, ch, 1:2], in1=a3, op0=ALU.mult, op1=ALU.add),
                                 reads=[BPS[pb], B["CP"], bACC[ai]], writes=[bACC[ai]])
                            S.op("vector", lambda e, p3=p3, a3=a3, ch=ch: e.scalar_tensor_tensor(out=a3, in0=p3[:, :, 0:128], scalar=CP[:, ch, 0:1], in1=a3, op0=ALU.mult, op1=ALU.add),
                                 reads=[BPS[pb], B["CP"], bACC[ai]], writes=[bACC[ai]])
                            accs.append(ai)
                        sgi = cn["sg"] % 2
                        cn["sg"] += 1
                        ag, av = accs
                        S.op("scalar", lambda e, sgi=sgi, ag=ag, nb_=nb_: e.activation(out=SG[sgi][:, 0:nb_ * 128], in_=ACC[ag][:, 0:nb_ * 128], func=AF.Silu),
                             reads=[bACC[ag]], writes=[bSG[sgi]])
                        S.op("gpsimd", lambda e, sgi=sgi, av=av, nb_=nb_, i=i, ta=ta, tb=tb: e.tensor_tensor(out=GQ[:, i, ta * 128:tb * 128], in0=SG[sgi][:, 0:nb_ * 128], in1=ACC[av][:, 0:nb_ * 128], op=ALU.mult),
                             reads=[bSG[sgi], bACC[av]], writes=[B["QT"]])
                for cg in range(4):
                    wb = cn["d"] % 2
                    cn["d"] += 1
                    S.dma("gpsimd", f"wd{wb}", lambda e, wb=wb, cg=cg, qf=qf: e.dma_start(out=WD[wb], in_=wd_view[:, qf * 11:(qf + 1) * 11, cg * 512:(cg + 1) * 512]),
                          reads=[BW["w_down"][l]], writes=[bWD[wb]])
                    for s in range(NB):
                        pb = 6 + cn["pd"] % 2
                        cn["pd"] += 1
                        for i in range(11):
                            S.op("tensor", lambda e, i=i, s=s, pb=pb, wb=wb: e.matmul(PS[pb][:, :], lhsT=GQ[:, i, s * 128:(s + 1) * 128], rhs=WD[wb][:, i, :], start=(i == 0), stop=(i == 10)),
                                 reads=[B["QT"], bWD[wb]], writes=[BPS[pb]], signal=(i == 10))
                        S.op("vector", lambda e, s=s, pb=pb, cg=cg: e.tensor_tensor(out=X[:, s, cg * 512:(cg + 1) * 512], in0=X[:, s, cg * 512:(cg + 1) * 512], in1=PS[pb][:, :], op=ALU.add),
                             reads=[BPS[pb], BXs[s]], writes=[BXs[s]])

        if stop == "x1":
            dump_X()
            return nc
        bY0, bY1, bJ = phase_bufs(["Y0", "Y1", "junkF"])
        Y = [r1(i * 4096, 4096, F32) for i in range(2)]
        bY = [bY0, bY1]
        junk = r1(8192, 2048)
        S.dma("sync", "gt", lambda e: e.dma_start(out=GT[:], in_=norm_final.broadcast_to([128, D])), writes=[B["GT"]])
        toks = []
        for s in range(NB):
            ss = SM[:, s:s + 1]
            rs = SM[:, 8 + s:9 + s]
            S.op("scalar", lambda e, s=s, ss=ss: e.activation(out=junk, in_=X[:, s, :], func=AF.Square, accum_out=ss), reads=[BXs[s]], writes=[bJ, smb(s)])
            rstd_from_ss(ss, rs, D, smb(s), smb(8 + s))
            S.op("vector", lambda e, s=s, rs=rs: e.scalar_tensor_tensor(out=Y[s % 2], in0=X[:, s, :], scalar=rs, in1=GT[:], op0=ALU.mult, op1=ALU.mult),
                 reads=[BXs[s], smb(8 + s), B["GT"]], writes=[bY[s % 2]])
            toks.append(S.dma("sync", f"out{s % 2}", lambda e, s=s: e.dma_start(out=out[s * 128:(s + 1) * 128, :], in_=Y[s % 2]), reads=[bY[s % 2]], writes=[bOut]))
        for t in toks:
            S.wait_tok("sync", t)
        S.emit()
    return nc


def _consts(c):
    bf = ml_dtypes.bfloat16
    i = np.arange(128)
    key = i[:, None, None, None]
    j = np.arange(8)[None, :, None, None]
    ds = np.arange(3)[None, None, :, None]
    q = i[None, None, None, :]
    delta = (8 * ds + c - j) * 128 + q - key
    mult = ((delta >= 0) & (delta <= 128)).astype(np.float32) \
        + ((delta >= 0) & (delta <= 512) & (delta % 4 == 0)).astype(np.float32) \
        + ((delta >= 0) & (delta <= 2048) & (delta % 16 == 0)).astype(np.float32)
    maskA = mult.reshape(128, 8 * 3 * 128).astype(bf)
    keyb = i[:, None, None]
    jb = np.arange(8)[None, :, None]
    qb = i[None, None, :]
    maskB = (((jb - c) * 128 + keyb) <= qb).astype(np.float32).reshape(128, 8 * 128).astype(bf)
    ident = np.eye(128, dtype=np.float32).astype(bf)
    triu = (i[None, :] >= i[:, None]).astype(np.float32)
    sel = np.zeros((128, 9), np.float32)
    if c >= 1:
        sel[:, c - 1] = 1.0
    else:
        sel[:, 8] = 1.0
    theta = np.float32(500000.0)
    pos = ((8 * np.arange(NB)[None, :] + c) * 128 + i[:, None]).astype(np.float32)

    def table(rot):
        inv = (1.0 / (theta ** (np.arange(0, rot, 2, dtype=np.float32) / np.float32(rot)))).astype(np.float32)
        ang = (pos[:, :, None] * inv[None, None, :]).astype(np.float32)
        return np.stack([np.cos(ang), np.sin(ang)], axis=2).astype(np.float32)
    ropeA = table(32).reshape(128, -1)
    ropeB = table(16).reshape(128, -1)
    return dict(maskA=maskA, maskB=maskB, ident=ident, triu=triu, sel=sel, ropeA=ropeA, ropeB=ropeB)


def make_in_maps(x, norm_mix, w_in, lambda_q1, lambda_k1, lambda_q2, lambda_k2, diff_subln,
                 sgu_ln_g, sgu_ln_b, sgu_w, sgu_b, w_out, norm_ffn, w_up, conv_w, conv_b, w_down, norm_final):
    f = lambda a: np.ascontiguousarray(np.asarray(a, dtype=np.float32))
    x = f(x)
    xb = x[0].reshape(NB, NC, 128, D)
    def shard(w):
        w = f(w)
        L, R, C = w.shape
        w4 = w.reshape(L, NC, R // NC, C)
        return [np.ascontiguousarray(w4[:, c].reshape(L * (R // NC), C)) for c in range(NC)]
    shards = {"w_in": shard(w_in), "w_out": shard(w_out), "w_up": shard(w_up), "w_down": shard(w_down)}
    lam4 = np.concatenate([f(lambda_q1), f(lambda_k1), f(lambda_q2), f(lambda_k2)], axis=1)
    sgu_wT = f(np.transpose(f(sgu_w), (0, 3, 1, 2))).reshape(DEPTH, 128, 512)
    sgu_bT = f(np.transpose(f(sgu_b), (0, 2, 1)))
    cw = np.transpose(f(conv_w).reshape(DEPTH, 3, 88, 128), (0, 3, 2, 1))
    cb = np.transpose(f(conv_b).reshape(DEPTH, 88, 128), (0, 2, 1))[..., None]
    conv_p = f(np.concatenate([cw, cb], axis=3)).reshape(DEPTH, 128, 88 * 4)
    common = dict(norm_mix=f(norm_mix), norm_ffn=f(norm_ffn), norm_final=f(norm_final).reshape(1, D), lam4=f(lam4),
                  diff_subln=f(diff_subln), sgu_ln_g=f(sgu_ln_g), sgu_ln_b=f(sgu_ln_b), sgu_wT=sgu_wT, sgu_bT=sgu_bT, conv_p=conv_p)
    in_maps = []
    for c in range(NC):
        m = dict(common)
        m["x"] = f(xb[:, c].reshape(NB * 128, D))
        for k in shards:
            m[k] = f(shards[k][c])
        m.update(_consts(c))
        in_maps.append(m)
    return in_maps


_PROG = {}


def kernel(**inputs):
    in_maps = make_in_maps(**inputs)
    key = (NL, STOP)
    if key not in _PROG:
        _PROG[key] = build_program(NL, STOP)
    res = run_bass_kernel_spmd(_PROG[key], in_maps, core_ids=list(range(NC)))
    outb = np.zeros((NB, NC, 128, D), np.float32)
    for c in range(NC):
        outb[:, c] = np.asarray(res.results[c]["out"]).reshape(NB, 128, D)
    return outb.reshape(1, SEQ, D)
```
